# Optimizing a Trainium2 kernel written in Bass

```python
import math
import jax, jax.numpy as jnp
from jax import lax
import numpy as np

D_MODEL = 1024
BATCH = 32
SEQ = 2048
DEPTH = 2
DEC_BATCH = 16
DEC_SEQ = 64
PAST_LEN = 1024

CHUNK = 64
MIX_WIDTH = D_MODEL
CONV_CH = MIX_WIDTH // 2
SSM_WIDTH = MIX_WIDTH - CONV_CH
CONV_K = 3
SSM_GROUP = 16
SSM_GROUPS = SSM_WIDTH // SSM_GROUP
SSM_STATE = 64
IN_PROJ_WIDTH = 3 * CONV_CH + SSM_WIDTH
N_MEM = 256
MEM_HEADS = 4
MEM_HEAD_DIM = D_MODEL // MEM_HEADS
D_FF = 4 * D_MODEL
EPS = 1e-6
DT_MIN = 1e-3
DT_MAX = 1e-1
A_RE_MAX = -1e-4

kernel_name = "hybrid_conv_s5_stream_step"


def _rmsnorm(x, g):
    x32 = x.astype(jnp.float32)
    inv = lax.rsqrt(jnp.mean(x32 * x32, axis=-1, keepdims=True) + EPS)
    return (x32 * inv * g.astype(jnp.float32)).astype(x.dtype)


def _short_conv(b_gate, c_gate, v, conv_state, conv_w):
    xin = c_gate * v
    s = xin.shape[1]
    xp = jnp.concatenate([conv_state.astype(xin.dtype), xin], axis=1)
    out = conv_w[0] * xp[:, 0:s]
    for k in range(1, CONV_K):
        out = out + conv_w[k] * xp[:, k:k + s]
    return b_gate * out, xp[:, -(CONV_K - 1):]


def _lin_combine(e1, e2):
    a1, b1 = e1
    a2, b2 = e2
    return (a1 * a2, a2 * b1 + b2)


def _s5(u, h0_re, h0_im, a_re, a_im, log_dt, b_re, b_im, c_re, c_im, d_skip, w_glu):
    f32 = jnp.float32
    bsz, s, _ = u.shape
    u32 = u.astype(f32).reshape(bsz, s, SSM_GROUPS, SSM_GROUP)
    lam = lax.complex(jnp.minimum(a_re.astype(f32), A_RE_MAX), a_im.astype(f32))
    dt = jnp.exp(log_dt.astype(f32))[:, None]
    a_bar = jnp.exp(lam * dt)
    b_mat = lax.complex(b_re.astype(f32), b_im.astype(f32))
    b_bar = ((a_bar - 1.0) / lam)[:, :, None] * b_mat
    bu = jnp.einsum('bsgh,gph->bsgp', u32.astype(jnp.complex64), b_bar)
    h0 = lax.complex(h0_re.astype(f32), h0_im.astype(f32))
    bu = bu.at[:, 0].add(a_bar * h0)
    a_el = jnp.broadcast_to(a_bar, (1, s) + a_bar.shape)
    _, h = lax.associative_scan(_lin_combine, (a_el, bu), axis=1)
    c_mat = lax.complex(c_re.astype(f32), c_im.astype(f32))
    y = jnp.real(jnp.einsum('bsgp,ghp->bsgh', h, c_mat)) + d_skip.astype(f32).reshape(SSM_GROUPS, SSM_GROUP) * u32
    y = jax.nn.gelu(y.reshape(bsz, s, SSM_WIDTH))
    y = y * jax.nn.sigmoid(y @ w_glu.astype(f32))
    h_last = h[:, -1]
    return (y.astype(u.dtype), jnp.real(h_last).astype(h0_re.dtype), jnp.imag(h_last).astype(h0_re.dtype))


def _mem_kv(mem, g_mem, w_k, w_v):
    b, m, _ = mem.shape
    mn = _rmsnorm(mem, g_mem)
    k = (mn @ w_k).reshape(b, m, MEM_HEADS, MEM_HEAD_DIM)
    v = (mn @ w_v).reshape(b, m, MEM_HEADS, MEM_HEAD_DIM)
    return k, v


def _mem_attend(xn, mem_k, mem_v, w_q, w_o):
    b, s, _ = xn.shape
    q = (xn @ w_q).reshape(b, s, MEM_HEADS, MEM_HEAD_DIM)
    scores = jnp.einsum('bshd,bmhd->bhsm', q, mem_k).astype(jnp.float32) / math.sqrt(MEM_HEAD_DIM)
    probs = jax.nn.softmax(scores, axis=-1).astype(xn.dtype)
    o = jnp.einsum('bhsm,bmhd->bshd', probs, mem_v).reshape(b, s, D_MODEL)
    return o @ w_o


def _layer(x, mem_k, mem_v, conv_state, h_re, h_im,
           g_mix, w_in, conv_w, a_re, a_im, log_dt, b_re, b_im, c_re, c_im, d_skip, w_glu,
           g_grp_a, g_grp_b, w_out, g_xattn, w_q, w_o, g_mlp, w_up, w_down):
    xn = _rmsnorm(x, g_mix)
    proj = xn @ w_in
    b_gate, c_gate, v, u = jnp.split(proj, [CONV_CH, 2 * CONV_CH, 3 * CONV_CH], axis=-1)
    y_a, new_conv = _short_conv(b_gate, c_gate, v, conv_state, conv_w)
    y_b, new_re, new_im = _s5(u, h_re, h_im, a_re, a_im, log_dt, b_re, b_im, c_re, c_im, d_skip, w_glu)
    y = jnp.concatenate([_rmsnorm(y_a, g_grp_a), _rmsnorm(y_b, g_grp_b)], axis=-1)
    x = x + y @ w_out
    x = x + _mem_attend(_rmsnorm(x, g_xattn), mem_k, mem_v, w_q, w_o)
    hdn = jnp.square(jax.nn.relu(_rmsnorm(x, g_mlp) @ w_up))
    x = x + hdn @ w_down
    return x, new_conv, new_re, new_im


def setup_inputs(seed: int = 0) -> dict:
    key = jax.random.key(seed)
    ks = jax.random.split(key, 40)
    f32 = jnp.float32

    def nrm(k, shape, fan_in):
        return jax.random.normal(k, shape, f32) * fan_in ** -0.5

    def gain(k, shape):
        return 1.0 + 0.02 * jax.random.normal(k, shape, f32)

    L, G, P, H = DEPTH, SSM_GROUPS, SSM_STATE, SSM_GROUP
    a_im_base = jnp.pi * jnp.arange(P, dtype=f32)
    return {
        "x_prompt": jax.random.normal(ks[0], (BATCH, SEQ, D_MODEL), f32),
        "x_sample": jax.random.normal(ks[1], (DEC_BATCH, DEC_SEQ, D_MODEL), f32),
        "mem_prompt": jax.random.normal(ks[2], (BATCH, N_MEM, D_MODEL), f32),
        "state_conv": jax.random.normal(ks[3], (L, DEC_BATCH, CONV_K - 1, CONV_CH), f32),
        "state_ssm_re": 0.5 * jax.random.normal(ks[4], (L, DEC_BATCH, G, P), f32),
        "state_ssm_im": 0.5 * jax.random.normal(ks[5], (L, DEC_BATCH, G, P), f32),
        "cache_mem_k": jax.random.normal(ks[6], (L, DEC_BATCH, N_MEM, MEM_HEADS, MEM_HEAD_DIM), f32),
        "cache_mem_v": jax.random.normal(ks[7], (L, DEC_BATCH, N_MEM, MEM_HEADS, MEM_HEAD_DIM), f32),
        "g_mix": gain(ks[8], (L, D_MODEL)),
        "w_in": nrm(ks[9], (L, D_MODEL, IN_PROJ_WIDTH), D_MODEL),
        "conv_w": nrm(ks[10], (L, CONV_K, CONV_CH), CONV_K),
        "ssm_a_re": -0.5 + 0.01 * jax.random.normal(ks[11], (L, G, P), f32),
        "ssm_a_im": a_im_base + 0.01 * jax.random.normal(ks[12], (L, G, P), f32),
        "ssm_log_dt": jax.random.uniform(ks[13], (L, G), f32, math.log(DT_MIN), math.log(DT_MAX)),
        "ssm_b_re": nrm(ks[14], (L, G, P, H), 2 * H),
        "ssm_b_im": nrm(ks[15], (L, G, P, H), 2 * H),
        "ssm_c_re": nrm(ks[16], (L, G, H, P), 2 * P),
        "ssm_c_im": nrm(ks[17], (L, G, H, P), 2 * P),
        "ssm_d": jax.random.normal(ks[18], (L, SSM_WIDTH), f32),
        "w_glu": nrm(ks[19], (L, SSM_WIDTH, SSM_WIDTH), SSM_WIDTH),
        "g_grp_a": gain(ks[20], (L, CONV_CH)),
        "g_grp_b": gain(ks[21], (L, SSM_WIDTH)),
        "w_out": nrm(ks[22], (L, MIX_WIDTH, D_MODEL), MIX_WIDTH),
        "g_xattn": gain(ks[23], (L, D_MODEL)),
        "g_mem": gain(ks[24], (L, D_MODEL)),
        "w_q": nrm(ks[25], (L, D_MODEL, D_MODEL), D_MODEL),
        "w_k": nrm(ks[26], (L, D_MODEL, D_MODEL), D_MODEL),
        "w_v": nrm(ks[27], (L, D_MODEL, D_MODEL), D_MODEL),
        "w_o": nrm(ks[28], (L, D_MODEL, D_MODEL), D_MODEL),
        "g_mlp": gain(ks[29], (L, D_MODEL)),
        "w_up": nrm(ks[30], (L, D_MODEL, D_FF), D_MODEL),
        "w_down": nrm(ks[31], (L, D_FF, D_MODEL), D_FF),
        "g_final": gain(ks[32], (D_MODEL,)),
    }


def reference(x_prompt, x_sample, mem_prompt, state_conv, state_ssm_re, state_ssm_im,
              cache_mem_k, cache_mem_v,
              g_mix, w_in, conv_w, ssm_a_re, ssm_a_im, ssm_log_dt, ssm_b_re, ssm_b_im,
              ssm_c_re, ssm_c_im, ssm_d, w_glu, g_grp_a, g_grp_b, w_out,
              g_xattn, g_mem, w_q, w_k, w_v, w_o, g_mlp, w_up, w_down, g_final):
    bp = x_prompt.shape[0]
    xp = x_prompt
    xs = x_sample
    conv_p, re_p, im_p, mk_p, mv_p = [], [], [], [], []
    conv_s, re_s, im_s = [], [], []
    zero_conv = jnp.zeros((bp, CONV_K - 1, CONV_CH), x_prompt.dtype)
    zero_h = jnp.zeros((bp, SSM_GROUPS, SSM_STATE), x_prompt.dtype)
    for l in range(DEPTH):
        lw = (g_mix[l], w_in[l], conv_w[l], ssm_a_re[l], ssm_a_im[l], ssm_log_dt[l],
              ssm_b_re[l], ssm_b_im[l], ssm_c_re[l], ssm_c_im[l], ssm_d[l], w_glu[l],
              g_grp_a[l], g_grp_b[l], w_out[l], g_xattn[l], w_q[l], w_o[l],
              g_mlp[l], w_up[l], w_down[l])
        mk, mv = _mem_kv(mem_prompt, g_mem[l], w_k[l], w_v[l])
        xp, c_new, r_new, i_new = _layer(xp, mk, mv, zero_conv, zero_h, zero_h, *lw)
        conv_p.append(c_new)
        re_p.append(r_new)
        im_p.append(i_new)
        mk_p.append(mk)
        mv_p.append(mv)
        xs, c_new, r_new, i_new = _layer(xs, cache_mem_k[l], cache_mem_v[l], state_conv[l],
                                         state_ssm_re[l], state_ssm_im[l], *lw)
        conv_s.append(c_new)
        re_s.append(r_new)
        im_s.append(i_new)
    y_prompt = _rmsnorm(xp, g_final)
    y_sample = _rmsnorm(xs, g_final)
    return (y_prompt, y_sample,
            jnp.stack(conv_p), jnp.stack(re_p), jnp.stack(im_p), jnp.stack(mk_p), jnp.stack(mv_p),
            jnp.stack(conv_s), jnp.stack(re_s), jnp.stack(im_s))
```

```python
import sys
import numpy as np
from contextlib import ExitStack
import concourse.bass as bass
import concourse.mybir as mybir
from concourse.bass_utils import run_bass_kernel_spmd

F32 = mybir.dt.float32
BF16 = mybir.dt.bfloat16
AF = mybir.ActivationFunctionType
ALU = mybir.AluOpType

NCORES = 8
D = 1024
DEPTH = 2
NMEM = 256
T = 8
EPS = 1e-6
NSLOT = 4
NDSEM = 12
EPOCH = 8000
PIECE = 4096

P_WIN, P_SB, P_SC, P_GLU, P_OUT, P_Q, P_O, P_UP, P_DOWN, P_K, P_V = 0, 4, 6, 8, 9, 11, 13, 15, 23, 31, 33
NPIECE = 35

GV_L = 56
def gvo(l, name):
    off = {"g_mix": 0, "g_xattn": 8, "g_mlp": 16, "g_mem": 24, "g_grp_a": 32, "g_grp_b": 36,
           "conv_w": 40, "ssm_d": 52}[name]
    return l * GV_L + off
GV_FINAL = 2 * GV_L
GV_MASK = GV_FINAL + 8
NGV = GV_MASK + 4


class Buf:
    __slots__ = ("ws", "wd", "rs", "rd", "const", "excl")

    def __init__(self, const=False, excl=False):
        self.excl = excl
        self.ws = {}
        self.wd = []
        self.rs = {}
        self.rd = []
        self.const = const


class Op:
    __slots__ = ("eng", "fn", "deps", "signal", "sem", "val", "cnt", "isdma", "site")


class Sched:
    def __init__(self, nc, es):
        self.nc = nc
        self.es = es
        self.ops = []
        self.h = {"pe": nc.tensor, "act": nc.scalar, "dve": nc.vector, "pool": nc.gpsimd, "sp": nc.sync}
        self.dq = {}
        for qn in ("sp", "pool"):
            self.dq[qn] = {"n": 0, "sems": [es.enter_context(nc.semaphore(f"dq{qn}{i}")) for i in range(NDSEM)],
                           "last": [None] * NDSEM}

    def op(self, eng, fn, r=(), w=()):
        o = Op()
        o.eng = eng
        o.fn = fn
        o.signal = False
        o.isdma = False
        o.cnt = 0
        f = sys._getframe(2)
        o.site = (f.f_lineno, f.f_back.f_lineno if f.f_back else 0, f.f_back.f_back.f_lineno if f.f_back and f.f_back.f_back else 0)
        deps = set()
        for b in r:
            deps.update(b.ws.values())
            deps.update(b.wd)
            if b.excl:
                deps.update(x for x in b.rs.values() if x.eng != eng)
                deps.update(b.rd)
        for b in w:
            deps.update(b.ws.values())
            deps.update(b.wd)
            deps.update(b.rs.values())
            deps.update(b.rd)
        o.deps = deps
        self.ops.append(o)
        self._r, self._w = r, w
        return o

    def _commit(self, o, r, w):
        dma = o.isdma
        for b in r:
            if b.const:
                continue
            if dma:
                b.rd.append(o)
            else:
                b.rs[o.eng] = o
        for b in w:
            b.rs = {}
            b.rd = []
            if dma:
                b.wd.append(o)
                if len(b.wd) > 4:
                    b.wd.pop(0)
            else:
                b.ws[o.eng] = o
        o.deps.discard(o)

    def c(self, eng, fn, r=(), w=()):
        o = self.op(eng, fn, r, w)
        self._commit(o, r, w)
        return o

    def dma(self, fn, r=(), w=(), q="sp"):
        o = self.op(q, fn, r, w)
        o.isdma = True
        dq = self.dq[q]
        k = dq["n"] % NDSEM
        prev = dq["last"][k]
        if prev is not None:
            o.deps.add(prev)
        o.sem = dq["sems"][k]
        o.val = 16 * (dq["n"] // NDSEM + 1)
        dq["last"][k] = o
        dq["n"] += 1
        o.signal = True
        self._commit(o, r, w)
        return o

    def emit(self):
        nc, es = self.nc, self.es
        for o in self.ops:
            for d in o.deps:
                if d.isdma:
                    continue
                if d.eng == "pe" and o.eng == "pe":
                    continue
                d.signal = True
        cnt = {"pe": 0, "act": 0, "dve": 0, "pool": 0, "sp": 0}
        for o in self.ops:
            if not o.isdma and o.signal:
                cnt[o.eng] += 1
                o.cnt = cnt[o.eng]
        sems = {}
        for e, n in cnt.items():
            sems[e] = [es.enter_context(nc.semaphore(f"s_{e}{i}")) for i in range(n // EPOCH + 1)]
        waited = {e: {} for e in self.h}
        for o in self.ops:
            h = self.h[o.eng]
            wt = waited[o.eng]
            best = {}
            for d in o.deps:
                if d.isdma:
                    key = ("d", id(d.sem))
                    if wt.get(key, 0) >= d.val:
                        continue
                    cur = best.get(key)
                    if cur is None or cur.val < d.val:
                        best[key] = d
                else:
                    if d.eng == "pe" and o.eng == "pe":
                        continue
                    key = ("c", d.eng)
                    if wt.get(key, 0) >= d.cnt:
                        continue
                    cur = best.get(key)
                    if cur is None or cur.cnt < d.cnt:
                        best[key] = d
            for key, d in best.items():
                if d.isdma:
                    h.wait_ge(d.sem, d.val)
                    wt[key] = d.val
                else:
                    ep = (d.cnt - 1) // EPOCH
                    h.wait_ge(sems[d.eng][ep], (d.cnt - 1) % EPOCH + 1)
                    wt[key] = d.cnt
            try:
                ins = o.fn(h)
            except Exception:
                print('EMIT FAILED at op site lines', o.site, o.eng)
                raise
            if o.isdma:
                ins.then_inc(o.sem, 16)
            elif o.signal:
                ins.then_inc(sems[o.eng][(o.cnt - 1) // EPOCH], 1)
        sp = nc.sync
        for dq in self.dq.values():
            for k in range(NDSEM):
                last = dq["last"][k]
                if last is not None:
                    sp.wait_ge(last.sem, last.val)
        for e in ("pe", "act", "dve", "pool"):
            if cnt[e] > 0:
                n = cnt[e]
                sp.wait_ge(sems[e][(n - 1) // EPOCH], (n - 1) % EPOCH + 1)


class _Stop(Exception):
    pass


STOP = None
PREFETCH = False
ONLY_A = False
LAG = 380.0


def build_program(S, with_sample=True):
    try:
        return _build_program(S, with_sample)
    except _Stop as ex:
        sc, es, nc = ex.args
        sc.emit()
        es.close()
        return nc


def _build_program(S, with_sample=True):
    nc = bass.Bass("TRN2", target_bir_lowering=False)
    es = ExitStack()
    NT = S // 256

    def din(name, shape, dt=F32):
        return nc.dram_tensor(name, list(shape), dt, kind="ExternalInput").ap()

    def dout(name, shape, dt=F32):
        return nc.dram_tensor(name, list(shape), dt, kind="ExternalOutput").ap()

    xp = din("xp", [4, S, D])
    xs = din("xs", [2, 64, D])
    memp = din("memp", [4, NMEM, D])
    ck = din("ck", [2, 2, NMEM, D])
    cv = din("cv", [2, 2, NMEM, D])
    w_in = din("w_in", [2, D, 2048])
    w_glu = din("w_glu", [2, 512, 512])
    w_out = din("w_out", [2, D, D])
    w_q = din("w_q", [2, D, D])
    w_k = din("w_k", [2, D, D])
    w_v = din("w_v", [2, D, D])
    w_o = din("w_o", [2, D, D])
    w_up = din("w_up", [2, D, 4096])
    w_down = din("w_down", [2, 4096, D])
    gv_d = din("gv", [128, NGV])
    s5a_d = din("s5a", [2, 128, 3, 16])
    s5b_d = din("s5b", [2, 128, 4, 5, 64])
    s5c_d = din("s5c", [2, 128, 2, 16, 16])
    stc_d = din("st_conv", [2, 128, 4, 2, 2])
    sts_d = din("st_ssm", [2, 128, 2, 2, 16])
    ident_d = din("ident", [128, 128])

    yp = dout("yp", [4, S, D])
    ys = dout("ys", [2, 64, D])
    convp = dout("convp", [2, 4, 2, 512])
    rep = dout("rep", [2, 4, 32, 64])
    imp = dout("imp", [2, 4, 32, 64])
    mkp = dout("mkp", [2, 4, NMEM, D])
    mvp = dout("mvp", [2, 4, NMEM, D])
    convs = dout("convs", [2, 2, 2, 512])
    res_ = dout("res", [2, 2, 32, 64])
    ims = dout("ims", [2, 2, 32, 64])

    WS = nc.dram_tensor("ws_scratch", [2, NPIECE, 128, PIECE], BF16, kind="Internal").ap()
    KVS = nc.dram_tensor("kv_scratch", [2, 6, 128, PIECE], BF16, kind="Internal").ap()
    WSB = [[Buf() for _ in range(NPIECE)] for _ in range(2)]
    KVB = [[Buf() for _ in range(6)] for _ in range(2)]

    sc = Sched(nc, es)

    def stop(tag):
        if STOP == tag:
            raise _Stop(sc, es, nc)

    def sb(name, shape, dt):
        return es.enter_context(nc.sbuf_tensor(name, list(shape), dt))

    X = sb("X", [128, 8, 1024], F32)
    XN = sb("XN", [128, 8, 1024], BF16)
    U = sb("U", [128, 32, 1024], BF16)
    RING = sb("RING", [128, NSLOT, PIECE], BF16)
    STI = sb("STI", [128, 1024], F32)
    STO = sb("STO", [128, 1024], F32)
    ST3 = sb("ST3", [128, 1024], F32)
    E2 = sb("E2", [128, 33, 2, 64], F32)
    XIN = sb("XIN", [128, 4, 4, 258], BF16)
    GV = sb("GV", [128, NGV], F32)
    IDN = sb("IDN", [128, 128], F32)
    ONES = sb("ONES", [128, 128], BF16)
    MASK = sb("MASK", [128, 512], F32)
    EPSB = sb("EPSB", [128, 1], F32)
    HPI = sb("HPI", [128, 1], F32)
    ZERO1 = sb("ZERO1", [128, 1], F32)
    CHA2 = sb("CHA2", [128, 2, 2, 2, 64], F32)
    CHA1 = sb("CHA1", [128, 2, 2, 2, 64], F32)
    CHA3 = sb("CHA3", [128, 2, 2, 2, 64], F32)
    INJ0 = sb("INJ0", [128, 2, 2, 64], F32)
    HFIN = sb("HFIN", [128, 2, 2, 64], F32)
    CTAIL = sb("CTAIL", [128, 2, 4, 4, 2], F32)
    TMPS = sb("TMPS", [128, 2, 3, 2, 32], F32)
    RS = sb("RS", [128, 2, 512], F32)
    GT = sb("GT", [128, 4, 512], F32)
    GTB = sb("GTB", [128, 2, 512], BF16)
    SMALL = sb("SMALL", [128, 8], F32)
    PS = es.enter_context(nc.psum_tensor("PS", [128, 8, 512], F32))

    bX = [[Buf() for _ in range(2)] for _ in range(8)]
    bXN = [[Buf() for _ in range(2)] for _ in range(8)]
    bU = [[Buf() for _ in range(2)] for _ in range(32)]
    bRING = [Buf() for _ in range(NSLOT)]
    bSTI, bSTO = Buf(), Buf()
    bST3 = Buf()
    stg_state = {"i": 0}
    bE2 = [Buf(), Buf()]
    bXIN = [Buf(), Buf()]
    bCONST = Buf()
    bCH = Buf()
    bINJ0 = [[Buf(), Buf()] for _ in range(2)]
    bHFIN = [[Buf(), Buf()] for _ in range(2)]
    bCTAIL = [[Buf(), Buf()] for _ in range(2)]
    bTMPS = [Buf(), Buf()]
    bRS = [Buf(), Buf()]
    bGT = [Buf() for _ in range(4)]
    bGTB = [Buf(), Buf()]
    bSMALL = Buf()
    bPS = [Buf(excl=True) for _ in range(8)]
    state = {"bank": 0, "slot": 0, "alt": 0}

    def nbank():
        i = state["bank"]
        state["bank"] = (i + 1) % 8
        return i

    def next_stg():
        i = stg_state["i"]
        stg_state["i"] = (i + 1) % 3
        return [(STI, bSTI), (STO, bSTO), (ST3, bST3)][i]

    def alt():
        state["alt"] ^= 1
        return "act" if state["alt"] else "dve"

    def DMA(out, in_, r=(), w=()):
        return sc.dma(lambda e: e.dma_start(out=out, in_=in_), r=r, w=w)

    def TT(eng, out, a, b_, op, r=(), w=()):
        return sc.c(eng, lambda e: e.tensor_tensor(out=out, in0=a, in1=b_, op=op), r=r, w=w)

    def TSS(eng, out, a, scalar, op, r=(), w=()):
        return sc.c(eng, lambda e: e.tensor_single_scalar(out=out, in_=a, scalar=scalar, op=op), r=r, w=w)

    def TS2(eng, out, a, s1, s2, op0, op1, r=(), w=()):
        return sc.c(eng, lambda e: e.tensor_scalar(out=out, in0=a, scalar1=s1, scalar2=s2, op0=op0, op1=op1), r=r, w=w)

    def STT(eng, out, a, scalar, b_, op0, op1, r=(), w=()):
        return sc.c(eng, lambda e: e.scalar_tensor_tensor(out=out, in0=a, scalar=scalar, in1=b_, op0=op0, op1=op1),
                    r=r, w=w)

    def ACTF(out, a, func, r=(), w=(), scale=1.0, bias=None, accum=None):
        kw = {}
        if bias is not None:
            kw["bias"] = bias
        if accum is not None:
            kw["accum_out"] = accum
        return sc.c("act", lambda e: e.activation(out=out, in_=a, func=func, scale=scale, **kw), r=r, w=w)

    def TC(eng, out, a, r=(), w=()):
        if eng == "act":
            return ACTF(out, a, AF.Copy, r=r, w=w)
        return sc.c(eng, lambda e: e.tensor_copy(out=out, in_=a), r=r, w=w)

    def MSET(eng, out, val, r=(), w=()):
        return sc.c(eng, lambda e: e.memset(out, val), r=r, w=w)

    def RCP(out, a, r=(), w=()):
        return sc.c("dve", lambda e: e.reciprocal(out=out, in_=a), r=r, w=w)

    def SCAN(out, d0, d1, r=(), w=()):
        return sc.c("dve", lambda e: e.tensor_tensor_scan(out=out, data0=d0, data1=d1, initial=0.0,
                                                          op0=ALU.mult, op1=ALU.add), r=r, w=w)

    def _emit_pe(e, items):
        ins = None
        for it in items:
            if it[0] == "mm":
                _, out, lhsT, rhs, st, sp_, tp = it
                if tp is None:
                    ins = e.matmul(out, lhsT, rhs, start=st, stop=sp_)
                else:
                    ins = e.matmul(out, lhsT, rhs, start=st, stop=sp_, tile_position=tp)
            else:
                _, out, in_, idn = it
                ins = e.transpose(out, in_, idn)
        return ins

    def PE(items, r=(), w=()):
        return sc.c("pe", lambda e: _emit_pe(e, items), r=r, w=w)

    def mm(out, lhsT, rhs, st, sp_, tp=None):
        return ("mm", out, lhsT, rhs, st, sp_, tp)

    def tr(out, in_, idn):
        return ("tr", out, in_, idn)

    def FENCE(bufs):
        MSET("dve", SMALL[:, 6:7], 0.0, r=bufs, w=bufs)

    DMA(GV[:], gv_d, w=[bCONST])
    DMA(IDN[:], ident_d, w=[bCONST])
    MSET("dve", ONES[:], 1.0, w=[bCONST])
    MSET("dve", MASK[:], 1.0, w=[bCONST])
    MSET("dve", MASK[:, 0:512:T], 0.0, w=[bCONST])
    MSET("dve", EPSB[:], EPS, w=[bCONST])
    MSET("dve", HPI[:], float(np.pi / 2), w=[bCONST])
    MSET("dve", ZERO1[:], 0.0, w=[bCONST])
    bCONST.const = True

    def gcol(c0, n=1):
        return GV[:, c0:c0 + n]

    def load_piece(src_ap, src_buf, nelem=PIECE):
        k = state["slot"]
        state["slot"] = (k + 1) % NSLOT
        DMA(RING[:, k, 0:nelem], src_ap[:, 0:nelem], r=[src_buf], w=[bRING[k]])
        return k

    def wpiece(l, idx, nelem=PIECE):
        return load_piece(WS[l, idx], WSB[l][idx], nelem)

    stop('const')
    Xf = X[:].rearrange("p a b -> p (a b)")
    XNf = XN[:].rearrange("p a b -> p (a b)")
    bStF = [Buf(), Buf()]
    bStB = [Buf(), Buf()]
    Uf = U[:].rearrange("p a b -> p (a b)").bitcast(F32)
    E2S = E2[:].rearrange("p a b c -> p (a b c)").bitcast(BF16)
    bW = Buf()
    bE2S = Buf()
    RW = [bW, bCONST]

    def PDMA(out, in_, r=(), w=()):
        return sc.dma(lambda e: e.dma_start(out=out, in_=in_), r=r, w=w, q="pool")

    def tt(out, a, b_, op):
        TT("dve", out, a, b_, op, r=RW, w=[bW])

    def tss(out, a, s, op):
        TSS("dve", out, a, s, op, r=RW, w=[bW])

    def act(out, a, func, scale=1.0, bias=None):
        ACTF(out, a, func, r=RW, w=[bW], scale=scale, bias=bias)

    def cp(out, a):
        TC("dve", out, a, r=RW, w=[bW])

    def cmul(o_r, o_i, a_r, a_i, b_r, b_i, t1, t2):
        tt(t1, a_r, b_r, ALU.mult)
        tt(t2, a_i, b_i, ALU.mult)
        tt(o_r, t1, t2, ALU.subtract)
        tt(t1, a_r, b_i, ALU.mult)
        tt(t2, a_i, b_r, ALU.mult)
        tt(o_i, t1, t2, ALU.add)

    def abar_calc(are_in, aim_in, ldt_in, n, off):
        o = {}
        names = ["are", "dt", "zr", "th", "c", "s", "t1", "t2", "t3", "er", "ern", "Ar", "Ai", "Vr", "Vi",
                 "nr", "ni", "den", "cr", "ci"]
        for i, nm in enumerate(names):
            o[nm] = Uf[:, off + i * n: off + (i + 1) * n]
        tss(o["are"], are_in, -1e-4, ALU.min)
        act(o["dt"], ldt_in, AF.Exp)
        tt(o["zr"], o["are"], o["dt"], ALU.mult)
        tt(o["th"], aim_in, o["dt"], ALU.mult)
        act(o["s"], o["th"], AF.Sin, scale=1.0 / 32)
        act(o["c"], o["th"], AF.Sin, scale=1.0 / 32, bias=HPI[:])
        for _ in range(5):
            tt(o["t1"], o["c"], o["c"], ALU.mult)
            tt(o["t2"], o["s"], o["s"], ALU.mult)
            tt(o["t3"], o["c"], o["s"], ALU.mult)
            tt(o["c"], o["t1"], o["t2"], ALU.subtract)
            tss(o["s"], o["t3"], 2.0, ALU.mult)
        act(o["er"], o["zr"], AF.Exp)
        act(o["ern"], o["zr"], AF.Exp, scale=-1.0)
        tt(o["Ar"], o["er"], o["c"], ALU.mult)
        tt(o["Ai"], o["er"], o["s"], ALU.mult)
        tt(o["Vr"], o["ern"], o["c"], ALU.mult)
        tt(o["t1"], o["ern"], o["s"], ALU.mult)
        tss(o["Vi"], o["t1"], -1.0, ALU.mult)
        tss(o["nr"], o["Ar"], -1.0, ALU.add)
        tt(o["t1"], o["are"], o["are"], ALU.mult)
        tt(o["t2"], aim_in, aim_in, ALU.mult)
        tt(o["den"], o["t1"], o["t2"], ALU.add)
        RCP(o["ni"], o["den"], r=RW, w=[bW])
        tt(o["t1"], o["nr"], o["are"], ALU.mult)
        tt(o["t2"], o["Ai"], aim_in, ALU.mult)
        tt(o["t3"], o["t1"], o["t2"], ALU.add)
        tt(o["cr"], o["t3"], o["ni"], ALU.mult)
        tt(o["t1"], o["Ai"], o["are"], ALU.mult)
        tt(o["t2"], o["nr"], aim_in, ALU.mult)
        tt(o["t3"], o["t1"], o["t2"], ALU.subtract)
        tt(o["ci"], o["t3"], o["ni"], ALU.mult)
        return o

    def store_chain(dst4, vr, vi):
        for s in range(4):
            for pt in range(2):
                TC("dve", dst4[:, 0, pt, s * 16:(s + 1) * 16], vr, r=RW, w=[bW, bCH])
            TSS("dve", dst4[:, 1, 0, s * 16:(s + 1) * 16], vi, -1.0, ALU.mult, r=RW, w=[bW, bCH])
            TC("dve", dst4[:, 1, 1, s * 16:(s + 1) * 16], vi, r=RW, w=[bW, bCH])

    O_A = 0
    O_KEEP = 5120
    O_INB = 7168
    O_PB = 9728
    O_ST = 11008
    O_TMP = 13056
    O_CT = 15104
    keep = {}
    stl = {}
    for l in range(2):
        PB = Uf[:, O_PB:O_PB + 1280].rearrange("p (a b c) -> p a b c", a=4, b=5)
        PDMA(PB, s5b_d[l], r=[bW], w=[bW])
        n = 256
        inb = Uf[:, O_INB + l * 1280:O_INB + (l + 1) * 1280]
        for k in range(5):
            cp(inb[:, k * n:(k + 1) * n].rearrange("p (a b) -> p a b", a=4), PB[:, :, k, :])
        o = abar_calc(inb[:, 0:n], inb[:, n:2 * n], inb[:, 2 * n:3 * n], n, O_A)
        kp = Uf[:, O_KEEP + l * 1024:O_KEEP + (l + 1) * 1024]
        for i, nm in enumerate(("cr", "ci", "Vr", "Vi")):
            cp(kp[:, i * n:(i + 1) * n], o[nm])
        keep[l] = (kp[:, 0:n], kp[:, n:2 * n], kp[:, 2 * n:3 * n], kp[:, 3 * n:4 * n], inb[:, 3 * n:4 * n], inb[:, 4 * n:5 * n])
        base = O_ST + l * 1024
        PA = Uf[:, base:base + 48].rearrange("p (a b) -> p a b", a=3)
        PDMA(PA, s5a_d[l], r=[bW], w=[bW])
        PC = Uf[:, base + 512:base + 1024].rearrange("p (a b c) -> p a b c", a=2, b=16)
        PDMA(PC, s5c_d[l], r=[bW], w=[bW])
        ina = Uf[:, base + 48:base + 96]
        for k in range(3):
            cp(ina[:, k * 16:(k + 1) * 16], PA[:, k, :])
        o2 = abar_calc(ina[:, 0:16], ina[:, 16:32], ina[:, 32:48], 16, base + 96)
        stl[l] = (o2["Ar"], o2["Ai"], PC)

    for l in range(2):
        n = 256
        cr0, ci0, Vr0, Vi0, bre, bim = keep[l]
        wr, wi, w2r, w2i, vr_, vi_, t1, t2 = [Uf[:, O_TMP + i * n:O_TMP + (i + 1) * n] for i in range(8)]
        BT = E2S[:, 0:8192].rearrange("p (pc kc j pt m) -> p pc kc j pt m", pc=2, kc=2, j=T, pt=2)
        cp(wr, cr0)
        cp(wi, ci0)
        for j in range(T):
            cmul(vr_, vi_, wr, wi, bre, bim, t1, t2)
            for pt, val in ((0, vr_), (1, vi_)):
                for g2 in range(2):
                    for kc in range(4):
                        dst = BT[:, kc // 2, kc % 2, j, pt, g2 * 64:(g2 + 1) * 64]
                        src = val[:, kc * 64:(kc + 1) * 64]
                        TSS("dve", dst, src, GV[:, GV_MASK + g2:GV_MASK + g2 + 1], ALU.mult, r=RW, w=[bE2S])
            if j < T - 1:
                cmul(w2r, w2i, wr, wi, Vr0, Vi0, t1, t2)
                cp(wr, w2r)
                cp(wi, w2i)
        for pc in range(2):
            PDMA(WS[l, P_SB + pc], E2S[:, pc * PIECE:(pc + 1) * PIECE], r=[bE2S], w=[WSB[l][P_SB + pc]])
        Ar_, Ai_, PC = stl[l]
        pr = Uf[:, O_CT:O_CT + 16]
        pi_ = Uf[:, O_CT + 16:O_CT + 32]
        p2r = Uf[:, O_CT + 32:O_CT + 48]
        p2i = Uf[:, O_CT + 48:O_CT + 64]
        t1 = Uf[:, O_CT + 64:O_CT + 80]
        t2 = Uf[:, O_CT + 80:O_CT + 96]
        cr_ = Uf[:, O_TMP:O_TMP + 256].rearrange("p (a b) -> p a b", a=16)
        ci_ = Uf[:, O_TMP + 256:O_TMP + 512].rearrange("p (a b) -> p a b", a=16)
        u1 = Uf[:, O_TMP + 512:O_TMP + 768].rearrange("p (a b) -> p a b", a=16)
        u2 = Uf[:, O_TMP + 768:O_TMP + 1024].rearrange("p (a b) -> p a b", a=16)
        CT = E2S[:, 0:8192].rearrange("p (pc q j pt m) -> p pc q j pt m", pc=2, q=8, j=T, pt=2)
        MSET("dve", pr, 1.0, r=RW, w=[bW])
        MSET("dve", pi_, 0.0, r=RW, w=[bW])
        for j in range(T + 1):
            if j < T:
                prb = pr.unsqueeze(2).to_broadcast([128, 16, 16])
                pib = pi_.unsqueeze(2).to_broadcast([128, 16, 16])
                tt(u1, PC[:, 0], prb, ALU.mult)
                tt(u2, PC[:, 1], pib, ALU.mult)
                tt(cr_, u1, u2, ALU.subtract)
                tt(u1, PC[:, 0], pib, ALU.mult)
                tt(u2, PC[:, 1], prb, ALU.mult)
                tt(ci_, u1, u2, ALU.add)
                for pt, val, sgn in ((0, cr_, 1.0), (1, ci_, -1.0)):
                    for g2 in range(2):
                        for pc in range(2):
                            dst = CT[:, pc, :, j, pt, g2 * 16:(g2 + 1) * 16]
                            src = val[:, pc * 8:(pc + 1) * 8, :]
                            TS2("dve", dst, src, GV[:, GV_MASK + 2 + g2:GV_MASK + 3 + g2], sgn, ALU.mult, ALU.mult,
                                r=RW, w=[bE2S])
            if j == 1:
                store_chain(CHA3[:, l], pr, pi_)
            if j == T - 1:
                store_chain(CHA1[:, l], pr, pi_)
            if j == T:
                store_chain(CHA2[:, l], pr, pi_)
            if j < T:
                cmul(p2r, p2i, pr, pi_, Ar_, Ai_, t1, t2)
                cp(pr, p2r)
                cp(pi_, p2i)
        for pc in range(2):
            PDMA(WS[l, P_SC + pc], E2S[:, pc * PIECE:(pc + 1) * PIECE], r=[bE2S], w=[WSB[l][P_SC + pc]])

    stop('tables0')
    RINGf = RING[:].rearrange("p a b -> p (a b)").bitcast(F32)
    GTb = GT[:].rearrange("p a b -> p (a b)").bitcast(BF16)
    STIb = STI[:].bitcast(BF16)
    STOb = STO[:].bitcast(BF16)
    stF = [Xf[:, 0:PIECE], Xf[:, PIECE:2 * PIECE], RINGf[:, 0:PIECE], RINGf[:, PIECE:2 * PIECE]]
    stB = [XNf[:, 0:PIECE], XNf[:, PIECE:2 * PIECE], GTb[:, 0:PIECE], None]
    bStF = [bStF[0], bStF[1], Buf(), Buf()]
    bStB = [bStB[0], bStB[1], Buf(), Buf()]
    cstate = {"i": 0, "pend": []}

    def flush_store(keep):
        while len(cstate["pend"]) > keep:
            (l, idx, i, nelem) = cstate["pend"].pop(0)
            if i < 3:
                DMA(WS[l, idx][:, 0:nelem], stB[i][:, 0:nelem], r=[bStB[i]], w=[WSB[l][idx]])
            else:
                h = min(nelem, 2048)
                DMA(WS[l, idx][:, 0:h], STIb[:, 0:h], r=[bStB[i]], w=[WSB[l][idx]])
                if nelem > 2048:
                    DMA(WS[l, idx][:, 2048:nelem], STOb[:, 0:nelem - 2048], r=[bStB[i]], w=[WSB[l][idx]])

    def convert(l, idx, src3, a, nelem=PIECE):
        i = cstate["i"] % 4
        cstate["i"] += 1
        st = stF[i][:, 0:nelem]
        DMA(st.rearrange("p (a b) -> p a b", a=a), src3, w=[bStF[i]])
        if i < 3:
            TC("act", stB[i][:, 0:nelem], st, r=[bStF[i]], w=[bStB[i]])
        else:
            h = min(nelem, 2048)
            TC("act", STIb[:, 0:h], st[:, 0:h], r=[bStF[i]], w=[bStB[i]])
            if nelem > 2048:
                TC("act", STOb[:, 0:nelem - 2048], st[:, 2048:nelem], r=[bStF[i]], w=[bStB[i]])
        cstate["pend"].append((l, idx, i, nelem))
        flush_store(2)

    def kn(w, l, n, ncols=512):
        return w[l].rearrange("(kc p) (n c) -> n p kc c", p=128, c=ncols)[n]

    for l in range(2):
        for n in range(2):
            convert(l, P_K + n, kn(w_k, l, n), 8)
        for n in range(2):
            convert(l, P_V + n, kn(w_v, l, n), 8)
    for l in range(2):
        for n in range(4):
            convert(l, P_WIN + n, kn(w_in, l, n), 8)
        convert(l, P_GLU, kn(w_glu, l, 0), 4, nelem=2048)
        for n in range(2):
            convert(l, P_OUT + n, kn(w_out, l, n), 8)
        for n in range(2):
            convert(l, P_Q + n, kn(w_q, l, n), 8)
        for n in range(2):
            convert(l, P_O + n, kn(w_o, l, n), 8)
        for n in range(8):
            convert(l, P_UP + n, kn(w_up, l, n), 8)
        for m in range(8):
            convert(l, P_DOWN + m, kn(w_down, l, m, 128), 32)

    stop('tables')
    KVST = U[:].rearrange("p a b -> p (a b)")
    bKVST = [Buf() for _ in range(4)]
    MNT = KVST[:, 4 * PIECE:4 * PIECE + 2 * 2048].rearrange("p (l kc m) -> p l kc m", l=2, kc=8)
    bMNT = Buf()
    TMPF = Xf[:, 0:1024]
    flush_store(0)
    FENCE([bW, bMNT, bE2S, bSTI, bSTO] + bKVST + bStF + bStB + bRING + bGT)
    bTMPF = bStF[0]

    def mem_norm_T(s):
        for mc in range(2):
            DMA(STI[:], memp[s, mc * 128:(mc + 1) * 128, :], w=[bSTI])
            MSET("dve", SMALL[:, 0:1], 0.0, w=[bSMALL])
            ACTF(TMPF, STI[:], AF.Square, r=[bSTI, bSMALL], w=[bTMPF, bSMALL], accum=SMALL[:, 0:1])
            ACTF(SMALL[:, 1:2], SMALL[:, 0:1], AF.Sqrt, r=[bSMALL, bCONST], w=[bSMALL], scale=1.0 / D, bias=EPSB[:])
            RCP(SMALL[:, 2:3], SMALL[:, 1:2], r=[bSMALL], w=[bSMALL])
            TSS("dve", TMPF, STI[:], SMALL[:, 2:3], ALU.mult, r=[bSTI, bSMALL], w=[bTMPF])
            for half in range(2):
                bk = nbank()
                PE([tr(PS[:, bk, k4 * 128:(k4 + 1) * 128], TMPF[:, (half * 4 + k4) * 128:(half * 4 + k4 + 1) * 128], IDN[:])
                    for k4 in range(4)], r=[bTMPF, bCONST], w=[bPS[bk]])
                for l in range(2):
                    for k4 in range(4):
                        kc = half * 4 + k4
                        ACTF(MNT[:, l, kc, mc * 128:(mc + 1) * 128], PS[:, bk, k4 * 128:(k4 + 1) * 128], AF.Copy,
                             r=[bPS[bk], bCONST], w=[bMNT], scale=gcol(gvo(l, "g_mem") + kc))

    def kv_tokmajor(l, s, wk_slots, out_d, kvst_off, st):
        for mc in range(2):
            for n in range(2):
                bk = nbank()
                k = wk_slots[n]
                PE([mm(PS[:, bk, :], MNT[:, l, kc, mc * 128:(mc + 1) * 128], RING[:, k, kc * 512:(kc + 1) * 512],
                       kc == 0, kc == 7) for kc in range(8)], r=[bMNT, bRING[k]], w=[bPS[bk]])
                ACTF(STO[:, n * 512:(n + 1) * 512], PS[:, bk, :], AF.Copy, r=[bPS[bk]], w=[bSTO])
                if kvst_off is not None:
                    o0 = st * PIECE + kvst_off + mc * 1024 + n * 512
                    TC("dve", KVST[:, o0:o0 + 512], PS[:, bk, :], r=[bPS[bk]], w=[bKVST[st]])
            DMA(out_d[l, s, mc * 128:(mc + 1) * 128, :], STO[:], r=[bSTO])

    def kT(l, s, wk_slots, st):
        for dc in range(8):
            bk = nbank()
            k = wk_slots[dc // 4]
            PE([mm(PS[:, bk, 0:256], RING[:, k, kc * 512 + (dc % 4) * 128: kc * 512 + (dc % 4) * 128 + 128],
                   MNT[:, l, kc, :], kc == 0, kc == 7) for kc in range(8)], r=[bMNT, bRING[k]], w=[bPS[bk]])
            TC(alt(), KVST[:, st * PIECE + dc * 256: st * PIECE + (dc + 1) * 256], PS[:, bk, 0:256],
               r=[bPS[bk]], w=[bKVST[st]])

    for s in range(4):
        mem_norm_T(s)
        stop('kv1')
        for l in range(2):
            st = (s * 2 + l) % 4
            wk = [wpiece(l, P_K + n) for n in range(2)]
            kv_tokmajor(l, s, wk, mkp, None, st)
            stop('kv2')
            kT(l, s, wk, st)
            stop('kv3')
            wv = [wpiece(l, P_V + n) for n in range(2)]
            kv_tokmajor(l, s, wv, mvp, 2048, st)
            DMA(KVS[l, s], KVST[:, st * PIECE:(st + 1) * PIECE], r=[bKVST[st]], w=[KVB[l][s]])
            stop('kv4')
    stop('kv5')
    if with_sample:
        for s in range(2):
            for l in range(2):
                st = (s * 2 + l) % 4
                for mc in range(2):
                    DMA(STI[:], ck[l, s, mc * 128:(mc + 1) * 128, :], w=[bSTI])
                    for half in range(2):
                        bk = nbank()
                        PE([tr(PS[:, bk, k4 * 128:(k4 + 1) * 128], STI[:, (half * 4 + k4) * 128:(half * 4 + k4 + 1) * 128], IDN[:])
                            for k4 in range(4)], r=[bSTI, bCONST], w=[bPS[bk]])
                        dst = KVST[:, st * PIECE + half * 1024: st * PIECE + (half + 1) * 1024].rearrange(
                            "p (k m) -> p k m", k=4)[:, :, mc * 128:(mc + 1) * 128]
                        src = PS[:, bk, :].rearrange("p (k m) -> p k m", k=4)
                        TC(alt(), dst, src, r=[bPS[bk]], w=[bKVST[st]])
                    DMA(STO[:], cv[l, s, mc * 128:(mc + 1) * 128, :], w=[bSTO])
                    o0 = st * PIECE + 2048 + mc * 1024
                    TC("dve", KVST[:, o0:o0 + 1024], STO[:], r=[bSTO], w=[bKVST[st]])
                DMA(KVS[l, 4 + s], KVST[:, st * PIECE:(st + 1) * PIECE], r=[bKVST[st]], w=[KVB[l][4 + s]])

    stop('kv')
    allb = ([bW, bMNT, bE2S, bSTI, bSTO, bST3, bSMALL] + bGT + bXIN + bE2 + bStF + bStB + bKVST + bRING + bPS
            + [b for row in bX for b in row] + [b for row in bXN for b in row] + [b for row in bU for b in row])
    FENCE(allb)

    PROJ0 = 0
    YMIX0 = 16
    GB0 = 24
    QB0, PB0, OB0 = 0, 8, 16
    HID0 = 0

    def stream(b, L, ntiles, x_d, y_d, kvidx, sample, outs_d):
        BW = 2 * L
        NCH = L // T
        NC2 = 2 * NCH
        c_base = b * 512
        cs = slice(c_base, c_base + BW)
        st_ = {"bank": 0, "slot": 0}
        LN = slice(32 * b, 32 * b + 32)
        SQ = slice(2 * b, 2 * b + 2)

        def nb():
            i = st_["bank"]
            st_["bank"] = (i + 1) % 4
            return 4 * b + i

        pref = {}

        def _issue(key):
            k = 2 * b + st_["slot"]
            st_["slot"] ^= 1
            if key[0] == "w":
                _, l_, idx, nelem = key
                DMA(RING[:, k, 0:nelem], WS[l_, idx][:, 0:nelem], r=[WSB[l_][idx]], w=[bRING[k]])
            else:
                _, l_, sq = key
                DMA(RING[:, k, :], KVS[l_, sq], r=[KVB[l_][sq]], w=[bRING[k]])
            return k

        def getp(key, nxt=None):
            if key in pref:
                k = pref.pop(key)
            else:
                assert not pref, (key, pref)
                k = _issue(key)
            if nxt is not None and PREFETCH:
                pref[nxt] = _issue(nxt)
            return k

        def wkey(l, idx, nelem=PIECE):
            return ("w", l, idx, nelem)

        def cmul_chain(dst, dst_w, src, tab, extra_r=(), eng="pool"):
            T1, T2 = TMPS[:, b, 1], TMPS[:, b, 2]
            rr = [bTMPS[b], bCH] + list(extra_r)
            TT(eng, T1, src, tab[:, 0, :, 0:32], ALU.mult, r=rr, w=[bTMPS[b]])
            TT(eng, T2[:, 0], src[:, 1], tab[:, 1, 0, 0:32], ALU.mult, r=rr, w=[bTMPS[b]])
            TT(eng, T2[:, 1], src[:, 0], tab[:, 1, 1, 0:32], ALU.mult, r=rr, w=[bTMPS[b]])
            TT(eng, dst, T1, T2, ALU.add, r=[bTMPS[b]], w=dst_w)

        for l in range(2):
            if sample:
                DMA(CTAIL[:, l, :, SQ, :], stc_d[l], r=[bCTAIL[l][b]], w=[bCTAIL[l][b]])
                DMA(TMPS[:, b, 0].rearrange("p a (s q) -> p a s q", s=2), sts_d[l], r=[bTMPS[b]], w=[bTMPS[b]])
                cmul_chain(INJ0[:, l, :, LN], [bINJ0[l][b]], TMPS[:, b, 0], CHA3[:, l])
            else:
                MSET("pool", CTAIL[:, l, :, SQ, :], 0.0, w=[bCTAIL[l][b]])
                MSET("pool", INJ0[:, l, :, LN], 0.0, w=[bINJ0[l][b]])
        yield

        def norm(src_is_x, gcol0, nchunks, src_chunk0=0, final=False):
            dim = nchunks * 128
            if src_is_x:
                src = X[:, 0:nchunks, cs]
                rb = [bX[k][b] for k in range(nchunks)]
            else:
                src = U[:, src_chunk0:src_chunk0 + nchunks, cs]
                rb = [bU[src_chunk0 + k][b] for k in range(nchunks)]
            sqb = [bXN[k][b] for k in range(nchunks)]
            ACTF(XN[:, 0:nchunks, cs], src, AF.Square, r=rb, w=sqb)
            bk = nb()
            PE([mm(PS[:, bk, 0:BW], ONES[:], XN[:, k, cs], k == 0, k == nchunks - 1) for k in range(nchunks)],
               r=sqb + [bCONST], w=[bPS[bk]])
            ACTF(RS[:, b, 0:BW], PS[:, bk, 0:BW], AF.Ln, r=[bPS[bk], bCONST], w=[bRS[b]], scale=1.0 / dim, bias=EPSB[:])
            ACTF(RS[:, b, 0:BW], RS[:, b, 0:BW], AF.Exp, r=[bRS[b]], w=[bRS[b]], scale=-0.5)
            for k in range(nchunks):
                g = gcol(gcol0 + k)
                if src_is_x:
                    if final:
                        dst, wb = X[:, k, cs], [bX[k][b]]
                    else:
                        dst, wb = XN[:, k, cs], [bXN[k][b]]
                    s_ = X[:, k, cs]
                    rb2 = [bX[k][b]]
                else:
                    dst, wb = U[:, src_chunk0 + k, cs], [bU[src_chunk0 + k][b]]
                    s_ = dst
                    rb2 = wb
                STT("dve", dst, s_, g, RS[:, b, 0:BW], ALU.mult, ALU.mult, r=rb2 + [bRS[b], bCONST], w=wb)

        def linear(pieces, n_in, src, src_bufs, outs, evac):
            for m in outs:
                k, coff = pieces(m)
                bk = nb()
                PE([mm(PS[:, bk, 0:BW], RING[:, k, coff(kc):coff(kc) + 128], src(kc), kc == 0, kc == n_in - 1)
                    for kc in range(n_in)],
                   r=[bRING[k]] + [src_bufs(kc) for kc in range(n_in)], w=[bPS[bk]])
                evac(m, bk)
                yield 0.25 * n_in

        def xn_src(kc):
            return XN[:, kc, cs]

        def xn_buf(kc):
            return bXN[kc][b]

        def ev_resid(m, bk):
            TT("dve", X[:, m, cs], X[:, m, cs], PS[:, bk, 0:BW], ALU.add, r=[bPS[bk], bX[m][b]], w=[bX[m][b]])

        def ev_copy_to(base):
            def f(m, bk):
                TC(alt(), U[:, base + m, cs], PS[:, bk, 0:BW], r=[bPS[bk]], w=[bU[base + m][b]])
            return f

        def btab(k, kc, r, j, pt):
            off = (((kc % 2) * T + j) * 2 + pt) * 128
            return RING[32 * r:32 * r + 32, k, off:off + 128]

        def v3(ch):
            return U[:, ch, cs].rearrange("p (s t) -> p s t", s=2)

        for ti in range(ntiles):
            t0 = ti * L
            for s in range(2):
                for tb in range(max(1, L // 128)):
                    nt = min(128, L)
                    c0 = c_base + s * L + tb * 128
                    SG, bSG = next_stg()
                    DMA(SG[0:nt, :], x_d[s, t0 + tb * 128: t0 + tb * 128 + nt, :], w=[bSG])
                    for half in range(2):
                        bk = nb()
                        PE([tr(PS[:, bk, k4 * 128:k4 * 128 + nt], SG[0:nt, (half * 4 + k4) * 128:(half * 4 + k4 + 1) * 128],
                               IDN[0:nt, 0:nt]) for k4 in range(4)], r=[bSG, bCONST], w=[bPS[bk]])
                        src = PS[:, bk, :].rearrange("p (k m) -> p k m", k=4)[:, :, 0:nt]
                        dst = X[:, half * 4:(half + 1) * 4, c0:c0 + nt]
                        TC(alt(), dst, src, r=[bPS[bk]], w=[bX[half * 4 + k][b] for k in range(4)])
                    yield

            for l in range(2):
                def mk_pieces(base, npc, after, l=l, order=None):
                    order_ = list(order) if order is not None else list(range(npc))
                    succ = {}
                    for i, n in enumerate(order_):
                        succ[n] = wkey(l, base + order_[i + 1]) if i + 1 < len(order_) else after
                    cache = {}

                    def f(m):
                        n = m // 4
                        if n not in cache:
                            cache[n] = getp(wkey(l, base + n), succ[n])
                        return cache[n], (lambda kc, m=m: kc * 512 + (m % 4) * 128)
                    return f, succ

                last_tile = (ti == ntiles - 1)
                if l == 0:
                    next_layer_first = wkey(1, P_WIN + 3)
                else:
                    next_layer_first = None if last_tile else wkey(0, P_WIN + 3)

                norm(True, gvo(l, "g_mix"), 8)
                yield 9.0
                win_piece, win_succ = mk_pieces(P_WIN, 4, None)
                win_succ[3] = wkey(l, P_SB)
                win_succ[1] = wkey(l, P_WIN + 2)
                win_succ[2] = wkey(l, P_WIN + 0)
                win_succ[0] = wkey(l, P_SB)
                ev_proj = ev_copy_to(PROJ0)
                yield from linear(win_piece, 8, xn_src, xn_buf, [12, 13, 14, 15], ev_proj)

                TC("dve", E2[:, 0, :, LN], INJ0[:, l, :, LN], r=[bINJ0[l][b]], w=[bE2[b]])
                for kc in range(4):
                    if kc % 2 == 0:
                        k = getp(wkey(l, P_SB + kc // 2), wkey(l, P_SB + 1) if kc == 0 else wkey(l, P_WIN + 1))
                    banks = [4 * b + r for r in range(4)]
                    items = []
                    for pt in range(2):
                        for j in range(T):
                            for r in range(4):
                                items.append(mm(PS[:, banks[r], pt * 64: pt * 64 + NC2], btab(k, kc, r, j, pt),
                                                U[32 * r:32 * r + 32, PROJ0 + 12 + kc, c_base + j: c_base + BW: T],
                                                j == 0, j == T - 1, (32 * r, 0)))
                    PE(items, r=[bRING[k], bU[PROJ0 + 12 + kc][b]], w=[bPS[x] for x in banks])
                    for pt in range(2):
                        src = PS[:, banks[0]:banks[0] + 4, pt * 64: pt * 64 + NC2].rearrange("p r (s c) -> p r s c", s=2)
                        dst = E2[:, 1:1 + NCH, pt, LN].rearrange("p c (s q) -> p q s c", s=2)[:, 4 * kc:4 * kc + 4, :, :]
                        ACTF(dst, src, AF.Copy, r=[bPS[x] for x in banks], w=[bE2[b]])
                    yield 9.0

                def chain_steps(l=l):
                    for c in range(NCH):
                        S0 = TMPS[:, b, 0]
                        TT("dve", S0, E2[:, c + 1, :, LN], E2[:, c, :, LN], ALU.add, r=[bE2[b], bTMPS[b]], w=[bTMPS[b]])
                        if c == NCH - 1:
                            cmul_chain(HFIN[:, l, :, LN], [bHFIN[l][b]], S0, CHA1[:, l], eng="dve")
                        cmul_chain(E2[:, c + 1, :, LN], [bE2[b]], S0, CHA2[:, l], eng="dve")
                        yield
                    TC("dve", INJ0[:, l, :, LN], E2[:, NCH, :, LN], r=[bE2[b]], w=[bINJ0[l][b]])

                def ev_proj_act(m, bk):
                    TC("act", U[:, PROJ0 + m, cs], PS[:, bk, 0:BW], r=[bPS[bk]], w=[bU[PROJ0 + m][b]])

                ch = chain_steps()
                for cost in linear(win_piece, 8, xn_src, xn_buf, [4, 5, 6, 7, 8, 9, 10, 11, 0, 1, 2, 3], ev_proj_act):
                    for _ in range(3):
                        next(ch, None)
                    yield cost + 3.0
                for _ in ch:
                    pass
                TC("dve", XIN[:, :, SQ, 0:2], CTAIL[:, l, :, SQ, :], r=[bCTAIL[l][b]], w=[bXIN[b]])
                for kc in range(4):
                    cg, vv, bg, ya = v3(PROJ0 + 4 + kc), v3(PROJ0 + 8 + kc), v3(PROJ0 + kc), v3(YMIX0 + kc)
                    rbs = [bU[PROJ0 + 4 + kc][b]]
                    vbs = [bU[PROJ0 + 8 + kc][b]]
                    bgs = [bU[PROJ0 + kc][b]]
                    yas = [bU[YMIX0 + kc][b]]
                    cw = gvo(l, "conv_w")
                    TT("dve", XIN[:, kc, SQ, 2:2 + L], cg, vv, ALU.mult, r=rbs + vbs + [bXIN[b]], w=[bXIN[b]])
                    TSS("dve", cg, XIN[:, kc, SQ, 0:L], gcol(cw + kc), ALU.mult, r=[bXIN[b], bCONST], w=rbs)
                    STT("dve", cg, XIN[:, kc, SQ, 1:L + 1], gcol(cw + 4 + kc), cg, ALU.mult, ALU.add,
                        r=[bXIN[b], bCONST] + rbs, w=rbs)
                    STT("dve", cg, XIN[:, kc, SQ, 2:L + 2], gcol(cw + 8 + kc), cg, ALU.mult, ALU.add,
                        r=[bXIN[b], bCONST] + rbs, w=rbs)
                    TT("pool", ya, bg, cg, ALU.mult, r=rbs + bgs, w=yas)
                    yield 4.0
                TC("dve", CTAIL[:, l, :, SQ, :], XIN[:, :, SQ, L:L + 2], r=[bXIN[b]], w=[bCTAIL[l][b]])

                yb_bank = 4 * b + 3
                slots = {"u": 0}

                def emitB(q, l=l, slots=slots):
                    kc, r = q // 4, q % 4
                    if q % 8 == 0:
                        slots["b"] = getp(wkey(l, P_SB + q // 8))
                        slots["c"] = getp(wkey(l, P_SC + q // 8))
                    k = slots["b"]
                    gbuf = q % 2
                    for pt in range(2):
                        bk = 4 * b + (slots["u"] % 3)
                        slots["u"] += 1
                        items = [mm(PS[:, bk, j:BW:T], btab(k, kc, r, j, pt),
                                    U[32 * r:32 * r + 32, PROJ0 + 12 + kc, c_base + j: c_base + BW: T], True, True, (32 * r, 0))
                                 for j in range(T)]
                        inj = E2[:, 0:NCH, pt, LN].rearrange("p c (s q) -> p q s c", s=2)[:, q, :, :]
                        PE(items, r=[bRING[k], bU[PROJ0 + 12 + kc][b]], w=[bPS[bk]])
                        tgt = PS[:, bk, 0:BW:T].rearrange("p (s c) -> p s c", s=2)
                        TT("dve", tgt, tgt, inj, ALU.add, r=[bPS[bk], bE2[b]], w=[bPS[bk]])
                        SCAN(U[:, GB0 + gbuf * 2 + pt, cs], MASK[:, 0:BW], PS[:, bk, 0:BW],
                             r=[bPS[bk], bCONST], w=[bU[GB0 + gbuf * 2 + pt][b]])

                def emitC(q, l=l, slots=slots):
                    kc, r = q // 4, q % 4
                    kcs = slots["c"]
                    gbuf = q % 2
                    items = []
                    for j in range(T):
                        for pt in range(2):
                            off = ((((q % 8) * T + j) * 2) + pt) * 32
                            items.append(mm(PS[32 * r:32 * r + 32, yb_bank, j:BW:T], RING[:, kcs, off:off + 32],
                                            U[:, GB0 + gbuf * 2 + pt, c_base + j: c_base + BW: T], pt == 0, pt == 1, (0, 32 * r)))
                    PE(items, r=[bRING[kcs], bU[GB0 + gbuf * 2][b], bU[GB0 + gbuf * 2 + 1][b]], w=[bPS[yb_bank]])
                    if r == 3:
                        yraw = GT[:, 2 * b, 0:BW]
                        tq = GT[:, 2 * b + 1, 0:BW]
                        by, bt_ = bGT[2 * b], bGT[2 * b + 1]
                        STT("dve", yraw, U[:, PROJ0 + 12 + kc, cs], gcol(gvo(l, "ssm_d") + kc), PS[:, yb_bank, 0:BW],
                            ALU.mult, ALU.add, r=[bU[PROJ0 + 12 + kc][b], bPS[yb_bank], bCONST], w=[by])
                        ACTF(tq, yraw, AF.Square, r=[by], w=[bt_], scale=float(np.sqrt(0.044715)))
                        STT("dve", tq, tq, 1.0, yraw, ALU.add, ALU.mult, r=[by, bt_], w=[bt_])
                        ACTF(tq, tq, AF.Sigmoid, r=[bt_], w=[bt_], scale=1.5957691216)
                        TT("pool", U[:, PROJ0 + 8 + kc, cs], yraw, tq, ALU.mult, r=[by, bt_], w=[bU[PROJ0 + 8 + kc][b]])

                for step in range(17):
                    c_first = (step == 8)
                    if step >= 1 and c_first:
                        emitC(step - 1)
                    if step < 16:
                        emitB(step)
                    if step >= 1 and not c_first:
                        emitC(step - 1)
                    yield 8.0

                kg = getp(wkey(l, P_GLU, 2048), wkey(l, P_OUT))

                def ev_glu(m, bk):
                    ACTF(GTB[:, b, 0:BW], PS[:, bk, 0:BW], AF.Sigmoid, r=[bPS[bk]], w=[bGTB[b]])
                    TT("dve", U[:, YMIX0 + 4 + m, cs], U[:, PROJ0 + 8 + m, cs], GTB[:, b, 0:BW], ALU.mult,
                       r=[bGTB[b], bU[PROJ0 + 8 + m][b]], w=[bU[YMIX0 + 4 + m][b]])
                yield from linear(lambda m, kg=kg: (kg, (lambda kc, m=m: kc * 512 + m * 128)), 4,
                                  lambda kc: U[:, PROJ0 + 8 + kc, cs], lambda kc: bU[PROJ0 + 8 + kc][b], [0, 1, 2, 3], ev_glu)

                norm(False, gvo(l, "g_grp_a"), 4, src_chunk0=YMIX0)
                yield 6.0
                norm(False, gvo(l, "g_grp_b"), 4, src_chunk0=YMIX0 + 4)
                yield 6.0

                yield from linear(mk_pieces(P_OUT, 2, wkey(l, P_Q))[0], 8, lambda kc: U[:, YMIX0 + kc, cs], lambda kc: bU[YMIX0 + kc][b],
                                  list(range(8)), ev_resid)

                norm(True, gvo(l, "g_xattn"), 8)
                yield 9.0
                yield from linear(mk_pieces(P_Q, 2, ("kv", l, kvidx[0]))[0], 8, xn_src, xn_buf, list(range(8)), ev_copy_to(QB0))

                for s in range(2):
                    scs = slice(c_base + s * L, c_base + (s + 1) * L)
                    kv = getp(("kv", l, kvidx[s]), ("kv", l, kvidx[1]) if s == 0 else wkey(l, P_O))
                    for h in range(4):
                        bk = nb()
                        items = []
                        for mc in range(2):
                            for dl in range(2):
                                dc = 2 * h + dl
                                items.append(mm(PS[:, bk, mc * L:(mc + 1) * L],
                                                RING[:, kv, dc * 256 + mc * 128: dc * 256 + (mc + 1) * 128],
                                                U[:, QB0 + dc, scs], dl == 0, dl == 1))
                        PE(items, r=[bRING[kv], bU[QB0 + 2 * h][b], bU[QB0 + 2 * h + 1][b]], w=[bPS[bk]])
                        pb = [bU[PB0 + 2 * h][b], bU[PB0 + 2 * h + 1][b]]
                        ACTF(U[:, PB0 + 2 * h:PB0 + 2 * h + 2, scs], PS[:, bk, 0:2 * L].rearrange("p (m t) -> p m t", m=2),
                             AF.Exp, r=[bPS[bk]], w=pb, scale=1.0 / 16.0)
                        bks = nb()
                        PE([mm(PS[:, bks, 0:L], ONES[:], U[:, PB0 + 2 * h + mc, scs], mc == 0, mc == 1) for mc in range(2)],
                           r=pb + [bCONST], w=[bPS[bks]])
                        ri = 2 * b + (h % 2)
                        ACTF(GT[:, ri, 0:L], PS[:, bks, 0:L], AF.Ln, r=[bPS[bks]], w=[bGT[ri]])
                        ACTF(GT[:, ri, 0:L], GT[:, ri, 0:L], AF.Exp, r=[bGT[ri]], w=[bGT[ri]], scale=-1.0)
                        bko = nb()
                        items = []
                        for dl in range(2):
                            dc = 2 * h + dl
                            for mc in range(2):
                                o0 = 2048 + mc * 1024 + dc * 128
                                items.append(mm(PS[:, bko, dl * L:(dl + 1) * L], RING[:, kv, o0:o0 + 128],
                                                U[:, PB0 + 2 * h + mc, scs], mc == 0, mc == 1))
                        PE(items, r=[bRING[kv]] + pb, w=[bPS[bko]])
                        TT("dve", U[:, OB0 + 2 * h:OB0 + 2 * h + 2, scs],
                           PS[:, bko, 0:2 * L].rearrange("p (m t) -> p m t", m=2),
                           GT[:, ri, 0:L].unsqueeze(1).to_broadcast([128, 2, L]), ALU.mult,
                           r=[bPS[bko], bGT[ri]], w=[bU[OB0 + 2 * h][b], bU[OB0 + 2 * h + 1][b]])
                        yield 8.0

                yield from linear(mk_pieces(P_O, 2, wkey(l, P_UP))[0], 8, lambda kc: U[:, OB0 + kc, cs], lambda kc: bU[OB0 + kc][b],
                                  list(range(8)), ev_resid)

                norm(True, gvo(l, "g_mlp"), 8)
                yield 9.0

                def ev_up(m, bk):
                    ACTF(U[:, HID0 + m, cs], PS[:, bk, 0:BW], AF.Relu, r=[bPS[bk]], w=[bU[HID0 + m][b]])
                    TT("pool", U[:, HID0 + m, cs], U[:, HID0 + m, cs], U[:, HID0 + m, cs], ALU.mult,
                       r=[bU[HID0 + m][b]], w=[bU[HID0 + m][b]])
                yield from linear(mk_pieces(P_UP, 8, wkey(l, P_DOWN))[0], 8, xn_src, xn_buf, list(range(32)), ev_up)

                def down_piece(m, l=l):
                    return getp(wkey(l, P_DOWN + m), wkey(l, P_DOWN + m + 1) if m < 7 else next_layer_first), (lambda kc: kc * 128)
                for m in range(8):
                    k, coff = down_piece(m)
                    bk = nb()
                    PE([mm(PS[:, bk, 0:BW], RING[:, k, coff(kc):coff(kc) + 128], U[:, HID0 + kc, cs], kc == 0, kc == 31)
                        for kc in range(32)],
                       r=[bRING[k]] + [bU[HID0 + kc][b] for kc in range(32)], w=[bPS[bk]])
                    ev_resid(m, bk)
                    yield 8.0

            norm(True, GV_FINAL, 8, final=True)
            yield 9.0
            for s in range(2):
                for tb in range(max(1, L // 128)):
                    nt = min(128, L)
                    c0 = c_base + s * L + tb * 128
                    SG, bSG = next_stg()
                    for half in range(2):
                        bk = nb()
                        PE([tr(PS[0:nt, bk, k4 * 128:(k4 + 1) * 128], X[:, half * 4 + k4, c0:c0 + nt], IDN[:]) for k4 in range(4)],
                           r=[bX[half * 4 + k][b] for k in range(4)] + [bCONST], w=[bPS[bk]])
                        TC(alt(), SG[0:nt, half * 512:(half + 1) * 512], PS[0:nt, bk, :], r=[bPS[bk]], w=[bSG])
                    DMA(y_d[s, t0 + tb * 128: t0 + tb * 128 + nt, :], SG[0:nt, :], r=[bSG])
                    yield 4.0

        conv_o, re_o, im_o = outs_d
        for l in range(2):
            bk = nb()
            PE([tr(PS[0:4, bk, kc * 128:(kc + 1) * 128], CTAIL[:, l, kc, SQ, :].rearrange("p s k -> p (s k)"), IDN[:])
                for kc in range(4)], r=[bCTAIL[l][b], bCONST], w=[bPS[bk]])
            TC("dve", STO[0:4, 0:512], PS[0:4, bk, :], r=[bPS[bk]], w=[bSTO])
            DMA(conv_o[l].rearrange("s k c -> (s k) c"), STO[0:4, 0:512], r=[bSTO])
            for pt, od in ((0, re_o), (1, im_o)):
                bk = nb()
                PE([tr(PS[0:32, bk, 0:128], HFIN[:, l, pt, LN], IDN[:])], r=[bHFIN[l][b], bCONST], w=[bPS[bk]])
                TC("dve", STO[0:32, 0:128], PS[0:32, bk, 0:128], r=[bPS[bk]], w=[bSTO])
                for s in range(2):
                    DMA(od[l, s].rearrange("(q g) p -> q (g p)", g=2), STO[s * 16:(s + 1) * 16, 0:128], r=[bSTO])
            yield

    def drive(gens, lag):
        live = [[i * float(lag), g] for i, g in enumerate(gens)]
        while live:
            live.sort(key=lambda x: x[0])
            ent = live[0]
            try:
                c = next(ent[1])
                if c == "resync":
                    ent[0] = min(x[0] for x in live)
                else:
                    ent[0] += (c if c is not None else 3.0)
            except StopIteration:
                live.remove(ent)

    gA = stream(0, 256, NT, xp[0:2], yp[0:2], [0, 1], False, (convp[:, 0:2], rep[:, 0:2], imp[:, 0:2]))
    gB = stream(1, 256, NT, xp[2:4], yp[2:4], [2, 3], False, (convp[:, 2:4], rep[:, 2:4], imp[:, 2:4]))

    def chain2(g1, g2):
        yield from g1
        yield "resync"
        for c in g2:
            yield (0.3 * c if isinstance(c, float) else c)
    if ONLY_A:
        drive([gA], 0)
    elif with_sample:
        gS = stream(0, 64, 1, xs, ys, [4, 5], True, (convs, res_, ims))
        drive([chain2(gA, gS), gB], LAG)
    else:
        drive([gA, gB], LAG)

    sc.emit()
    es.close()
    return nc


def _fm(vec, nch):
    return np.ascontiguousarray(vec.reshape(nch, 128).T)


def _pack_gv(inp):
    gv = np.zeros((128, NGV), np.float32)
    for l in range(2):
        for name, nch in (("g_mix", 8), ("g_xattn", 8), ("g_mlp", 8), ("g_mem", 8), ("g_grp_a", 4), ("g_grp_b", 4)):
            gv[:, gvo(l, name):gvo(l, name) + nch] = _fm(np.asarray(inp[name][l]), nch)
        for k in range(3):
            gv[:, gvo(l, "conv_w") + 4 * k: gvo(l, "conv_w") + 4 * k + 4] = _fm(np.asarray(inp["conv_w"][l, k]), 4)
        gv[:, gvo(l, "ssm_d"):gvo(l, "ssm_d") + 4] = _fm(np.asarray(inp["ssm_d"][l]), 4)
    gv[:, GV_FINAL:GV_FINAL + 8] = _fm(np.asarray(inp["g_final"]), 8)
    p = np.arange(128)
    gv[:, GV_MASK + 0] = ((p // 16) % 2 == 0)
    gv[:, GV_MASK + 1] = ((p // 16) % 2 == 1)
    gv[:, GV_MASK + 2] = (p < 64)
    gv[:, GV_MASK + 3] = (p >= 64)
    return gv


def _layouts(inp):
    a_re, a_im, ldt = [np.asarray(inp[k], np.float32) for k in ("ssm_a_re", "ssm_a_im", "ssm_log_dt")]
    b_re, b_im, c_re, c_im = [np.asarray(inp[k], np.float32) for k in ("ssm_b_re", "ssm_b_im", "ssm_c_re", "ssm_c_im")]
    s5a = np.zeros((2, 128, 3, 16), np.float32)
    s5b = np.zeros((2, 128, 4, 5, 64), np.float32)
    s5c = np.zeros((2, 128, 2, 16, 16), np.float32)
    for l in range(2):
        ld = np.repeat(ldt[l][:, None], 64, axis=1)
        for k, arr in enumerate((a_re[l], a_im[l], ld)):
            s5a[l, :, k, :] = arr.reshape(16, 2, 64).transpose(1, 2, 0).reshape(128, 16)
            s5b[l, :, :, k, :] = np.repeat(arr, 16, axis=0).reshape(4, 128, 64).transpose(1, 0, 2)
        for k, arr in enumerate((b_re[l], b_im[l])):
            s5b[l, :, :, 3 + k, :] = arr.transpose(0, 2, 1).reshape(512, 64).reshape(4, 128, 64).transpose(1, 0, 2)
        for k, arr in enumerate((c_re[l], c_im[l])):
            s5c[l, :, k] = arr.reshape(16, 2, 16, 64).transpose(1, 3, 0, 2).reshape(128, 16, 16)
    return s5a, s5b, s5c


_CACHE = {}


def kernel(**inp):
    x_prompt = np.asarray(inp["x_prompt"], np.float32)
    x_sample = np.asarray(inp["x_sample"], np.float32)
    B, S, _ = x_prompt.shape
    assert B == 32 and S % 256 == 0
    key = S
    if key not in _CACHE:
        _CACHE[key] = build_program(S)
    nc = _CACHE[key]
    gv = _pack_gv(inp)
    s5a, s5b, s5c = _layouts(inp)
    ident = np.eye(128, dtype=np.float32)
    shared = {k: np.ascontiguousarray(np.asarray(inp[k], np.float32)) for k in
              ("w_in", "w_glu", "w_out", "w_q", "w_k", "w_v", "w_o", "w_up", "w_down")}
    mem = np.asarray(inp["mem_prompt"], np.float32)
    cmk = np.asarray(inp["cache_mem_k"], np.float32).reshape(2, 16, NMEM, D)
    cmv = np.asarray(inp["cache_mem_v"], np.float32).reshape(2, 16, NMEM, D)
    stc = np.asarray(inp["state_conv"], np.float32)
    sre = np.asarray(inp["state_ssm_re"], np.float32)
    sim = np.asarray(inp["state_ssm_im"], np.float32)
    in_maps = []
    for c in range(NCORES):
        ps_, ss_ = slice(4 * c, 4 * c + 4), slice(2 * c, 2 * c + 2)
        st_conv = np.ascontiguousarray(stc[:, ss_].reshape(2, 2, 2, 4, 128).transpose(0, 4, 3, 1, 2))
        st_ssm = np.zeros((2, 128, 2, 2, 16), np.float32)
        for k, arr in enumerate((sre, sim)):
            st_ssm[:, :, k] = arr[:, ss_].reshape(2, 2, 16, 2, 64).transpose(0, 3, 4, 1, 2).reshape(2, 128, 2, 16)
        m = dict(shared)
        m.update({"xp": np.ascontiguousarray(x_prompt[ps_]), "xs": np.ascontiguousarray(x_sample[ss_]),
                  "memp": np.ascontiguousarray(mem[ps_]), "ck": np.ascontiguousarray(cmk[:, ss_]),
                  "cv": np.ascontiguousarray(cmv[:, ss_]), "gv": gv, "s5a": s5a, "s5b": s5b, "s5c": s5c,
                  "st_conv": st_conv, "st_ssm": st_ssm, "ident": ident})
        in_maps.append(m)
    res = run_bass_kernel_spmd(nc, in_maps, core_ids=list(range(NCORES)))
    R = res.results

    def cat(name, axis):
        return np.concatenate([np.asarray(r[name], np.float32) for r in R], axis=axis)
    y_prompt = cat("yp", 0)
    y_sample = cat("ys", 0)
    conv_p = cat("convp", 1)
    re_p = cat("rep", 1)
    im_p = cat("imp", 1)
    mk_p = cat("mkp", 1).reshape(2, 32, NMEM, 4, 256)
    mv_p = cat("mvp", 1).reshape(2, 32, NMEM, 4, 256)
    conv_s = cat("convs", 1)
    re_s = cat("res", 1)
    im_s = cat("ims", 1)
    return (y_prompt, y_sample, conv_p, re_p, im_p, mk_p, mv_p, conv_s, re_s, im_s)
```
